# Optimizing a Trainium2 kernel written in Bass

```python
import math
import jax, jax.numpy as jnp
from jax import lax
import numpy as np

D_MODEL = 2048
BATCH = 4
SEQ = 4096
DEPTH = 4

N_A = DEPTH // 2
N_B = DEPTH - N_A

CHUNK = 128
A_WIDTH = D_MODEL
A_GROUPS = 16
A_GROUP_DIM = A_WIDTH // A_GROUPS

N_HEADS = 16
HEAD_DIM = 128
ATT_WIDTH = N_HEADS * HEAD_DIM
Q_BLOCK = 128

FFN_DIM = 5632
CONV_W = 3

EPS = 1e-6

kernel_name = "hybrid_gmlp_fox_yoco_convffn"


def rmsnorm(x, g):
    xf = x.astype(jnp.float32)
    r = lax.rsqrt(jnp.mean(xf * xf, axis=-1, keepdims=True) + EPS)
    return (xf * r).astype(x.dtype) * g


def chunked_gmlp(xn, w_in, v_norm, w_s, b_s, w_out):
    B, S, _ = xn.shape
    z = jax.nn.gelu(xn @ w_in, approximate=False)
    u, v = jnp.split(z, 2, axis=-1)
    v = rmsnorm(v, v_norm)
    v = v.reshape(B, S // CHUNK, CHUNK, A_GROUPS, A_GROUP_DIM)
    causal = jnp.tril(jnp.ones((CHUNK, CHUNK), dtype=bool))
    w = jnp.where(causal[None], w_s, jnp.zeros_like(w_s))
    mixed = jnp.einsum('gts,bnsgc->bntgc', w, v) + b_s.T[:, :, None]
    gated = u * mixed.reshape(B, S, A_WIDTH)
    return gated @ w_out


def conv_ffn(xn, w_up, conv_w, conv_b, w_down):
    S = xn.shape[1]
    h = xn @ w_up
    hp = jnp.pad(h, ((0, 0), (CONV_W - 1, 0), (0, 0)))
    h = sum(hp[:, k:k + S] * conv_w[k] for k in range(CONV_W)) + conv_b
    gate, val = jnp.split(h, 2, axis=-1)
    return (jax.nn.silu(gate) * val) @ w_down


def fox_shared_kv(h, kv_norm, w_kvf, b_f, k_norm):
    B, S, _ = h.shape
    xn = rmsnorm(h, kv_norm)
    kvf = xn @ w_kvf
    k = kvf[..., :ATT_WIDTH].reshape(B, S, N_HEADS, HEAD_DIM)
    v = kvf[..., ATT_WIDTH:2 * ATT_WIDTH].reshape(B, S, N_HEADS, HEAD_DIM)
    f = kvf[..., 2 * ATT_WIDTH:] + b_f
    k = rmsnorm(k, k_norm).transpose(0, 2, 1, 3)
    v = v.transpose(0, 2, 1, 3)
    log_f = jax.nn.log_sigmoid(f.astype(jnp.float32))
    c = jnp.cumsum(log_f, axis=1).transpose(0, 2, 1)
    return k, v, c


def forgetting_attention(xn, w_qg, q_norm, w_out, k, v, c):
    B, S, _ = xn.shape
    qg = xn @ w_qg
    q, g = jnp.split(qg, 2, axis=-1)
    q = rmsnorm(q.reshape(B, S, N_HEADS, HEAD_DIM), q_norm).transpose(0, 2, 1, 3)
    scale = HEAD_DIM ** -0.5
    outs = []
    for i in range(S // Q_BLOCK):
        lo, hi = i * Q_BLOCK, (i + 1) * Q_BLOCK
        qb = q[:, :, lo:hi]
        kb = k[:, :, :hi]
        vb = v[:, :, :hi]
        s = (jnp.einsum('bhqd,bhkd->bhqk', qb, kb).astype(jnp.float32) * scale
             + c[:, :, lo:hi, None] - c[:, :, None, :hi])
        mask = (lo + jnp.arange(Q_BLOCK))[:, None] >= jnp.arange(hi)[None, :]
        s = jnp.where(mask, s, -jnp.inf)
        p = jax.nn.softmax(s, axis=-1).astype(vb.dtype)
        outs.append(jnp.einsum('bhqk,bhkd->bhqd', p, vb))
    o = jnp.concatenate(outs, axis=2).transpose(0, 2, 1, 3).reshape(B, S, ATT_WIDTH)
    o = o * jax.nn.sigmoid(g)
    return o @ w_out


def setup_inputs(seed: int = 0) -> dict:
    key = jax.random.key(seed)
    ks = jax.random.split(key, 24)
    f32 = jnp.float32

    def nrm(k, shape, scale):
        return jax.random.normal(k, shape, f32) * scale

    def gain(k, shape):
        return 1.0 + 0.02 * jax.random.normal(k, shape, f32)

    return {
        "x": jax.random.normal(ks[0], (BATCH, SEQ, D_MODEL), f32),
        "a_norm": gain(ks[1], (N_A, D_MODEL)),
        "a_w_in": nrm(ks[2], (N_A, D_MODEL, 2 * A_WIDTH), D_MODEL ** -0.5),
        "a_v_norm": gain(ks[3], (N_A, A_WIDTH)),
        "a_w_s": nrm(ks[4], (N_A, A_GROUPS, CHUNK, CHUNK), 0.5 * CHUNK ** -0.5),
        "a_b_s": 1.0 + 0.1 * jax.random.normal(ks[5], (N_A, A_GROUPS, CHUNK), f32),
        "a_w_out": nrm(ks[6], (N_A, A_WIDTH, D_MODEL), A_WIDTH ** -0.5),
        "kv_norm": gain(ks[7], (D_MODEL,)),
        "w_kvf": nrm(ks[8], (D_MODEL, 2 * ATT_WIDTH + N_HEADS), D_MODEL ** -0.5),
        "b_f": jax.random.uniform(ks[9], (N_HEADS,), f32, 2.0, 5.0),
        "k_norm": gain(ks[10], (HEAD_DIM,)),
        "b_norm": gain(ks[11], (N_B, D_MODEL)),
        "b_w_qg": nrm(ks[12], (N_B, D_MODEL, 2 * ATT_WIDTH), D_MODEL ** -0.5),
        "q_norm": gain(ks[13], (N_B, HEAD_DIM)),
        "b_w_out": nrm(ks[14], (N_B, ATT_WIDTH, D_MODEL), ATT_WIDTH ** -0.5),
        "f_norm": gain(ks[15], (DEPTH, D_MODEL)),
        "f_w_up": nrm(ks[16], (DEPTH, D_MODEL, 2 * FFN_DIM), D_MODEL ** -0.5),
        "f_conv_w": jnp.zeros((DEPTH, CONV_W, 2 * FFN_DIM), f32).at[:, CONV_W - 1].set(1.0)
                    + nrm(ks[17], (DEPTH, CONV_W, 2 * FFN_DIM), 0.3),
        "f_conv_b": nrm(ks[18], (DEPTH, 2 * FFN_DIM), 0.01),
        "f_w_down": nrm(ks[19], (DEPTH, FFN_DIM, D_MODEL), FFN_DIM ** -0.5),
        "final_norm": gain(ks[20], (D_MODEL,)),
    }


def reference(x, a_norm, a_w_in, a_v_norm, a_w_s, a_b_s, a_w_out,
              kv_norm, w_kvf, b_f, k_norm,
              b_norm, b_w_qg, q_norm, b_w_out,
              f_norm, f_w_up, f_conv_w, f_conv_b, f_w_down, final_norm):
    h = x
    k_sh = v_sh = c_sh = None
    for l in range(DEPTH):
        if l < N_A:
            h = h + chunked_gmlp(rmsnorm(h, a_norm[l]), a_w_in[l], a_v_norm[l],
                                 a_w_s[l], a_b_s[l], a_w_out[l])
        else:
            j = l - N_A
            h = h + forgetting_attention(rmsnorm(h, b_norm[j]), b_w_qg[j], q_norm[j],
                                         b_w_out[j], k_sh, v_sh, c_sh)
        h = h + conv_ffn(rmsnorm(h, f_norm[l]), f_w_up[l], f_conv_w[l],
                         f_conv_b[l], f_w_down[l])
        if l == N_A - 1:
            k_sh, v_sh, c_sh = fox_shared_kv(h, kv_norm, w_kvf, b_f, k_norm)
    return rmsnorm(h, final_norm)
```

```python
import numpy as np
import concourse.bass as bass
import concourse.mybir as mybir
from concourse.bass_utils import run_bass_kernel_spmd
from contextlib import ExitStack

F32 = mybir.dt.float32
BF16 = mybir.dt.bfloat16
AF = mybir.ActivationFunctionType
ALU = mybir.AluOpType

D = 2048
FF = 5632
NH = 16
HD = 128
EPS = 1e-6
NFC = FF // 128
SHIFT = 30.0


class Buf:
    __slots__ = ("name", "lw", "rd", "sem", "dc")

    def __init__(self, name):
        self.name = name
        self.lw = None
        self.rd = {}
        self.sem = None
        self.dc = 0


class Eng:
    def __init__(self, name, e, sem):
        self.name = name
        self.e = e
        self.sem = sem
        self.count = 0
        self.seen = {}


class KB:
    def __init__(self, nc, es):
        self.nc = nc
        self.es = es
        self.nsem = 0
        self.pe = Eng("pe", nc.tensor, self.newsem("e_pe"))
        self.act = Eng("act", nc.scalar, self.newsem("e_act"))
        self.dve = Eng("dve", nc.vector, self.newsem("e_dve"))
        self.pool = Eng("pool", nc.gpsimd, self.newsem("e_pool"))
        self.sp = Eng("sp", nc.sync, self.newsem("e_sp"))
        self.engs = [self.pe, self.act, self.dve, self.pool, self.sp]
        self.dbufs = []
        self.kept = []
        self.freesems = []
        self.allbufs = []

    def newsem(self, name):
        self.nsem += 1
        return self.es.enter_context(self.nc.semaphore(name))

    def buf(self, name, dma=False, keep=False):
        b = Buf(name)
        if dma:
            if self.freesems and not keep:
                b.sem, b.dc = self.freesems.pop()
            else:
                b.sem = self.newsem("d_%s_%d" % (name, self.nsem))
            if keep:
                self.kept.append(b)
            else:
                self.dbufs.append(b)
        self.allbufs.append(b)
        return b

    def _wait(self, eng, evs):
        for sem, val in evs:
            k = id(sem)
            if eng.seen.get(k, 0) >= val:
                continue
            eng.e.wait_ge(sem, val)
            eng.seen[k] = val

    def _deps(self, reads, writes):
        evs = []
        for b in reads:
            if b.lw is not None:
                evs.append(b.lw)
        for b in writes:
            if b.lw is not None:
                evs.append(b.lw)
            evs.extend(b.rd.values())
        return evs

    def _update(self, ev, reads, writes):
        for b in writes:
            b.lw = ev
            b.rd = {}
        for b in reads:
            if b in writes:
                continue
            k = id(ev[0])
            old = b.rd.get(k)
            if old is None or old[1] < ev[1]:
                b.rd[k] = ev

    def op(self, eng, fns, reads=(), writes=()):
        self._wait(eng, self._deps(reads, writes))
        if callable(fns):
            fns = [fns]
        inst = None
        for f in fns:
            inst = f()
        inst.then_inc(eng.sem, 1)
        eng.count += 1
        self._update((eng.sem, eng.count), reads, writes)

    def dma(self, q, out, in_, reads, writes, sb, **kw):
        self._wait(q, self._deps(reads, writes))
        if not isinstance(out, (list, tuple)):
            out, in_ = [out], [in_]
        for o, i in zip(out, in_):
            sb.dc += 16
            q.e.dma_start(out=o, in_=i, **kw).then_inc(sb.sem, 16)
        self._update((sb.sem, sb.dc), reads, writes)

    def barrier(self):
        evs = [(e.sem, e.count) for e in self.engs if e.count > 0]
        evs += [(b.sem, b.dc) for b in self.dbufs + self.kept if b.dc > 0]
        for e in self.engs:
            self._wait(e, evs)
        for b in self.allbufs:
            b.lw = None
            b.rd = {}
        for b in self.dbufs:
            self.freesems.append((b.sem, b.dc))
            b.sem = None
        self.dbufs = []


class Ctx:
    pass


def setup_consts(kb, cx, es):
    nc = kb.nc
    cx.identf = es.enter_context(nc.sbuf_tensor("identf", [128, 128], F32))
    cx.ident = es.enter_context(nc.sbuf_tensor("ident", [128, 128], BF16))
    cx.onesb = es.enter_context(nc.sbuf_tensor("onesb", [128, 128], BF16))
    cx.onesf = es.enter_context(nc.sbuf_tensor("onesf", [128, 128], F32))
    cx.trif = es.enter_context(nc.sbuf_tensor("trif", [128, 128], F32))
    cx.cb = kb.buf("consts")
    kb.op(kb.pool, lambda: nc.gpsimd.memset(cx.identf[:], 0.0), writes=[cx.cb])
    kb.op(kb.pool, lambda: nc.gpsimd.affine_select(
        out=cx.identf[:], in_=cx.identf[:], pattern=[[-1, 128]],
        compare_op=ALU.not_equal, fill=1.0, base=0, channel_multiplier=1), writes=[cx.cb])
    kb.op(kb.pool, lambda: nc.gpsimd.memset(cx.onesf[:], 1.0), writes=[cx.cb])
    kb.op(kb.pool, lambda: nc.gpsimd.memset(cx.trif[:], 1.0), writes=[cx.cb])
    kb.op(kb.pool, lambda: nc.gpsimd.affine_select(
        out=cx.trif[:], in_=cx.trif[:], pattern=[[1, 128]],
        compare_op=ALU.is_ge, fill=0.0, base=0, channel_multiplier=-1), writes=[cx.cb])
    kb.op(kb.dve, lambda: nc.vector.tensor_copy(out=cx.ident[:], in_=cx.identf[:]),
          reads=[cx.cb], writes=[cx.cb])
    kb.op(kb.dve, lambda: nc.vector.tensor_copy(out=cx.onesb[:], in_=cx.onesf[:]),
          reads=[cx.cb], writes=[cx.cb])
    kb.barrier()


def rstd_from_ss(kb, ss, sd, rs, n, b_ss, b_sd, b_rs):
    nc = kb.nc
    kb.op(kb.act, lambda: nc.scalar.activation(out=sd, in_=ss, func=AF.Sqrt,
                                               bias=EPS_AP[0], scale=1.0 / n),
          reads=[b_ss], writes=[b_sd])
    kb.op(kb.dve, lambda: nc.vector.reciprocal(out=rs, in_=sd), reads=[b_sd], writes=[b_rs])


EPS_AP = [None]


def norm_phase(kb, cx, tiles, gain_ap, XNT, xb, name):
    nc = kb.nc
    with ExitStack() as es:
        GB = es.enter_context(nc.sbuf_tensor(name + "GB", [128, D], F32))
        HT = [es.enter_context(nc.sbuf_tensor(name + "HT%d" % i, [128, D], F32)) for i in range(2)]
        XN = [es.enter_context(nc.sbuf_tensor(name + "XN%d" % i, [128, D], BF16)) for i in range(2)]
        JK = es.enter_context(nc.sbuf_tensor(name + "JK", [128, D], BF16))
        ST = [es.enter_context(nc.sbuf_tensor(name + "ST%d" % i, [128, 4], F32)) for i in range(2)]
        PT = [es.enter_context(nc.psum_tensor(name + "PT%d" % i, [128, 8, 128], BF16)) for i in range(4)]
        b_gb = kb.buf("gb", dma=True)
        b_ht = [kb.buf("ht%d" % i, dma=True) for i in range(2)]
        b_xn = [kb.buf("xn%d" % i) for i in range(2)]
        b_jk = kb.buf("jk")
        b_st = [[kb.buf("st%d_%d" % (i, j)) for j in range(3)] for i in range(2)]
        b_pt = [kb.buf("pt%d" % i) for i in range(4)]
        kb.dma(kb.sp, GB[:], gain_ap.partition_broadcast(128), [], [b_gb], b_gb)
        for ti, (ap, col0, ncols, sc0) in enumerate(tiles):
            dbuf = None
            s = ti % 2
            kb.dma(kb.sp, HT[s][:], ap, [dbuf] if dbuf is not None else [], [b_ht[s]], b_ht[s])
            kb.op(kb.act, lambda: nc.scalar.activation(out=JK[:], in_=HT[s][:], func=AF.Square,
                                                       accum_out=ST[s][:, 0:1]),
                  reads=[b_ht[s]], writes=[b_jk, b_st[s][0]])
            rstd_from_ss(kb, ST[s][:, 0:1], ST[s][:, 1:2], ST[s][:, 2:3], D,
                         b_st[s][0], b_st[s][1], b_st[s][2])
            kb.op(kb.dve, lambda: nc.vector.scalar_tensor_tensor(
                out=XN[s][:], in0=HT[s][:], scalar=ST[s][:, 2:3], in1=GB[:],
                op0=ALU.mult, op1=ALU.mult),
                reads=[b_ht[s], b_st[s][2], b_gb], writes=[b_xn[s]])
            for hh in range(2):
                p = (ti * 2 + hh) % 4
                kb.op(kb.pe, [(lambda k=k: nc.tensor.transpose(
                    out=PT[p][:, k - hh * 8, :], in_=XN[s][:, k * 128:(k + 1) * 128],
                    identity=cx.ident[:])) for k in range(hh * 8, hh * 8 + 8)],
                    reads=[b_xn[s]], writes=[b_pt[p]])
                eng = kb.act if hh == 0 else kb.dve
                if hh == 0:
                    kb.op(kb.act, lambda: nc.scalar.copy(
                        out=XNT[:, 0:8, col0:col0 + ncols], in_=PT[p][:, :, sc0:sc0 + ncols]),
                        reads=[b_pt[p]], writes=[xb[col0]])
                else:
                    kb.op(kb.dve, lambda: nc.vector.tensor_copy(
                        out=XNT[:, 8:16, col0:col0 + ncols], in_=PT[p][:, :, sc0:sc0 + ncols]),
                        reads=[b_pt[p]], writes=[xb[col0]])
        kb.barrier()


def load_w(kb, dst, src, b, nsplit=1):
    n = dst.shape[-1]
    step = n // nsplit
    kb.dma(kb.pool, [dst[:, :, i * step:(i + 1) * step] for i in range(nsplit)],
           [src[:, :, i * step:(i + 1) * step] for i in range(nsplit)], [], [b], b)


def outproj_phase(kb, cx, SRC, nch, ntt, w_ap, h_ap, name, src_reads, tok0=0):
    nc = kb.nc
    DB = 1024
    with ExitStack() as es:
        W = [es.enter_context(nc.sbuf_tensor(name + "W%d" % i, [128, nch, DB], BF16)) for i in range(1)]
        HT = [es.enter_context(nc.sbuf_tensor(name + "HT%d" % i, [128, DB], F32)) for i in range(3)]
        PS = [es.enter_context(nc.psum_tensor(name + "PS%d" % i, [128, 512], F32)) for i in range(4)]
        b_w = kb.buf("opw", dma=True)
        b_ht = [kb.buf("opht%d" % i, dma=True) for i in range(3)]
        b_ps = [kb.buf("opps%d" % i) for i in range(4)]
        wv = w_ap.rearrange("(c p) d -> p c d", p=128)
        it = 0
        for dh in range(D // DB):
            load_w(kb, W[0], wv[:, :, dh * DB:(dh + 1) * DB], b_w, nsplit=2)
            for i in range(ntt):
                s = it % 3
                pp = (it % 2) * 2
                it += 1
                hsl = h_ap[tok0 + i * 128: tok0 + (i + 1) * 128, dh * DB:(dh + 1) * DB]
                kb.dma(kb.sp, HT[s][:], hsl, [], [b_ht[s]], b_ht[s])
                fns = []
                for c in range(nch):
                    for j in range(2):
                        fns.append(lambda c=c, j=j: nc.tensor.matmul(
                            PS[pp + j][:], lhsT=SRC[:, c, i * 128:(i + 1) * 128],
                            rhs=W[0][:, c, j * 512:(j + 1) * 512], start=(c == 0), stop=(c == nch - 1)))
                kb.op(kb.pe, fns, reads=[b_w] + list(src_reads), writes=[b_ps[pp], b_ps[pp + 1]])
                for j in range(2):
                    kb.op(kb.dve, lambda j=j: nc.vector.tensor_tensor(
                        out=HT[s][:, j * 512:(j + 1) * 512], in0=HT[s][:, j * 512:(j + 1) * 512],
                        in1=PS[pp + j][:], op=ALU.add),
                        reads=[b_ps[pp + j], b_ht[s]], writes=[b_ht[s]])
                kb.dma(kb.sp, hsl, HT[s][:], [b_ht[s]], [], b_ht[s])
        kb.barrier()


def ffn_phase(kb, cx, l, NT, tok0=0, halo=None):
    nc = kb.nc
    NTT = NT // 128
    NQ = NT // 512
    h = cx.h
    GROUPS = [(0, 8), (8, 16), (16, 24), (24, 32), (32, 40), (40, 44)]
    GMAX = 8
    with ExitStack() as es:
        XNT = es.enter_context(nc.sbuf_tensor("fXNT", [128, 16, NT + 2], BF16))
        CW = es.enter_context(nc.sbuf_tensor("fCW", [128, 2 * NFC, 3], F32))
        CBt = es.enter_context(nc.sbuf_tensor("fCB", [128, 2 * NFC], F32))
        xb = {i * 128: kb.buf("xnt%d" % i) for i in range(NTT + 1)}
        b_c = kb.buf("fconst", dma=True, keep=True)
        kb.dma(kb.sp, CW[:], cx.f_cw[l], [], [b_c], b_c)
        kb.dma(kb.sp, CBt[:], cx.f_cb[l], [], [b_c], b_c)
        tiles = std_tiles(h, NTT, tok0)
        if halo is None:
            tiles.append((cx.halo[l], NT, 2, 0))
        else:
            tiles.append((halo[0], NT, 2, halo[1]))
        norm_phase(kb, cx, tiles, cx.f_norm[l], XNT, xb, "fn")
        xall = list(xb.values())
        with ExitStack() as es2:
            ACTB = es2.enter_context(nc.sbuf_tensor("fACT", [128, GMAX, NT], BF16))
            WD = es2.enter_context(nc.sbuf_tensor("fWD", [128, GMAX, D], BF16))
            WU = [[es2.enter_context(nc.sbuf_tensor("fWU%d_%d" % (a, i), [128, 16, 256], BF16))
                   for i in range(2)] for a in range(2)]
            RAW = [[es2.enter_context(nc.sbuf_tensor("fRAW%d_%d" % (a, i), [128, 514], F32))
                    for i in range(2)] for a in range(2)]
            ACC = [[es2.enter_context(nc.sbuf_tensor("fACC%d_%d" % (a, i), [128, 512], F32))
                    for i in range(2)] for a in range(2)]
            SG = [es2.enter_context(nc.sbuf_tensor("fSG%d" % i, [128, 512], F32)) for i in range(2)]
            HT = [es2.enter_context(nc.sbuf_tensor("fHT%d" % i, [128, D], F32)) for i in range(2)]
            PU = [[es2.enter_context(nc.psum_tensor("fPU%d_%d" % (a, i), [128, 512], F32))
                   for i in range(2)] for a in range(2)]
            PH = es2.enter_context(nc.psum_tensor("fPH", [128, 512], F32))
            PD = [es2.enter_context(nc.psum_tensor("fPD%d" % i, [128, 512], F32)) for i in range(2)]
            b_act = [kb.buf("actb%d" % i) for i in range(NQ)]
            b_wd = kb.buf("wd", dma=True)
            b_wu = [[kb.buf("wu%d_%d" % (a, i), dma=True) for i in range(2)] for a in range(2)]
            b_raw = [[kb.buf("raw%d_%d" % (a, i)) for i in range(2)] for a in range(2)]
            b_acc = [[kb.buf("acc%d_%d" % (a, i)) for i in range(2)] for a in range(2)]
            b_sg = [kb.buf("sg%d" % i) for i in range(2)]
            b_ht = [kb.buf("fht%d" % i, dma=True) for i in range(2)]
            b_pu = [[kb.buf("pu%d_%d" % (a, i)) for i in range(2)] for a in range(2)]
            b_ph = [kb.buf("ph%d" % i) for i in range(2)]
            b_pd = [kb.buf("pd%d" % i) for i in range(2)]
            wup = cx.f_w_up[l].rearrange("(k p) n -> p k n", p=128)
            wdn = cx.f_w_down[l].rearrange("(g p) d -> p g d", p=128)
            step = 0
            dstep = 0
            blocks = [c0 for (f0, f1) in GROUPS for c0 in range(f0, f1, 2)]

            def load_wu(bi):
                c0 = blocks[bi]
                ws = bi % 2
                load_w(kb, WU[0][ws], wup[:, :, c0 * 128:c0 * 128 + 256], b_wu[0][ws])
                load_w(kb, WU[1][ws], wup[:, :, FF + c0 * 128:FF + c0 * 128 + 256], b_wu[1][ws])

            load_wu(0)
            bi = -1
            for (f0, f1) in GROUPS:
                G = f1 - f0
                load_w(kb, WD[:, 0:G, :], wdn[:, f0:f1, :], b_wd, nsplit=2)
                for c0 in range(f0, f1, 2):
                    bi += 1
                    ws = bi % 2
                    if bi + 1 < len(blocks):
                        load_wu(bi + 1)
                    for ci in range(2):
                        c = c0 + ci
                        hs = c % 2
                        fns = []
                        for a in range(2):
                            for k in range(16):
                                fns.append(lambda a=a, k=k: nc.tensor.matmul(
                                    PH[:, hs * 8 + a * 2: hs * 8 + a * 2 + 2],
                                    lhsT=WU[a][ws][:, k, ci * 128:(ci + 1) * 128],
                                    rhs=XNT[:, k, NT:NT + 2], start=(k == 0), stop=(k == 15)))
                        kb.op(kb.pe, fns, reads=[b_wu[0][ws], b_wu[1][ws], xb[NT]], writes=[b_ph[hs]])
                        for T in range(NQ):
                            s = step % 2
                            step += 1
                            for a in range(2):
                                fns = [(lambda a=a, k=k: nc.tensor.matmul(
                                    PU[a][s][:], lhsT=WU[a][ws][:, k, ci * 128:(ci + 1) * 128],
                                    rhs=XNT[:, k, T * 512:(T + 1) * 512], start=(k == 0), stop=(k == 15)))
                                    for k in range(16)]
                                kb.op(kb.pe, fns, reads=[b_wu[a][ws]] + [xb[(4 * T + q) * 128] for q in range(4)],
                                      writes=[b_pu[a][s]])
                            for a in range(2):
                                cc = a * NFC + c
                                if T == 0:
                                    kb.op(kb.act, lambda a=a: nc.scalar.copy(
                                        out=RAW[a][s][:, 0:2], in_=PH[:, hs * 8 + a * 2: hs * 8 + a * 2 + 2]),
                                        reads=[b_ph[hs]], writes=[b_raw[a][s]])
                                else:
                                    kb.op(kb.act, lambda a=a: nc.scalar.copy(
                                        out=RAW[a][s][:, 0:2], in_=RAW[a][1 - s][:, 512:514]),
                                        reads=[b_raw[a][1 - s]], writes=[b_raw[a][s]])
                                kb.op(kb.act, lambda a=a: nc.scalar.copy(out=RAW[a][s][:, 2:514], in_=PU[a][s][:]),
                                      reads=[b_pu[a][s]], writes=[b_raw[a][s]])
                                kb.op(kb.act, lambda a=a, cc=cc: nc.scalar.activation(
                                    out=ACC[a][s][:], in_=RAW[a][s][:, 2:514], func=AF.Identity,
                                    bias=CBt[:, cc:cc + 1], scale=CW[:, cc, 2:3]),
                                    reads=[b_raw[a][s], b_c], writes=[b_acc[a][s]])
                                e1 = kb.dve
                                kb.op(e1, lambda a=a, cc=cc, e1=e1: e1.e.scalar_tensor_tensor(
                                    out=ACC[a][s][:], in0=RAW[a][s][:, 1:513], scalar=CW[:, cc, 1:2],
                                    in1=ACC[a][s][:], op0=ALU.mult, op1=ALU.add),
                                    reads=[b_raw[a][s], b_c, b_acc[a][s]], writes=[b_acc[a][s]])
                                kb.op(e1, lambda a=a, cc=cc, e1=e1: e1.e.scalar_tensor_tensor(
                                    out=ACC[a][s][:], in0=RAW[a][s][:, 0:512], scalar=CW[:, cc, 0:1],
                                    in1=ACC[a][s][:], op0=ALU.mult, op1=ALU.add),
                                    reads=[b_raw[a][s], b_c, b_acc[a][s]], writes=[b_acc[a][s]])
                            kb.op(kb.act, lambda: nc.scalar.activation(out=SG[s][:], in_=ACC[0][s][:], func=AF.Silu),
                                  reads=[b_acc[0][s]], writes=[b_sg[s]])
                            kb.op(kb.dve, lambda: nc.vector.tensor_tensor(
                                out=ACTB[:, c - f0, T * 512:(T + 1) * 512], in0=SG[s][:], in1=ACC[1][s][:],
                                op=ALU.mult),
                                reads=[b_sg[s], b_acc[1][s]], writes=[b_act[T]])
                for i in range(NTT):
                    hs = dstep % 2
                    dstep += 1
                    kb.dma(kb.sp, HT[hs][:], h[tok0 + i * 128: tok0 + (i + 1) * 128, :], [], [b_ht[hs]], b_ht[hs])
                    for dh in range(2):
                        banks = [(PU[0][0], b_pu[0][0]), (PU[0][1], b_pu[0][1])] if dh == 0 else \
                                [(PD[0], b_pd[0]), (PD[1], b_pd[1])]
                        fns = []
                        for g in range(G):
                            for j in range(2):
                                fns.append(lambda g=g, j=j, banks=banks: nc.tensor.matmul(
                                    banks[j][0][:], lhsT=ACTB[:, g, i * 128:(i + 1) * 128],
                                    rhs=WD[:, g, dh * 1024 + j * 512: dh * 1024 + (j + 1) * 512],
                                    start=(g == 0), stop=(g == G - 1)))
                        kb.op(kb.pe, fns, reads=[b_wd, b_act[i // 4]], writes=[banks[0][1], banks[1][1]])
                        for j in range(2):
                            sl = slice(dh * 1024 + j * 512, dh * 1024 + (j + 1) * 512)
                            kb.op(kb.dve, lambda j=j, sl=sl, banks=banks: nc.vector.tensor_tensor(
                                out=HT[hs][:, sl], in0=HT[hs][:, sl], in1=banks[j][0][:], op=ALU.add),
                                reads=[banks[j][1], b_ht[hs]], writes=[b_ht[hs]])
                    kb.dma(kb.sp, h[tok0 + i * 128: tok0 + (i + 1) * 128, :], HT[hs][:], [b_ht[hs]], [], b_ht[hs])
            kb.barrier()
        kb.barrier()


def headnorm(kb, cx, PQ, b_pq, gain_col, out_ap, b_out, tmp, name):
    nc = kb.nc
    KF, KSQ, SD, RS, PSS, b_kf, b_ksq, b_sd, b_rs, b_pss = tmp
    kb.op(kb.act, lambda: nc.scalar.copy(out=KF[:], in_=PQ[:]), reads=[b_pq], writes=[b_kf])
    kb.op(kb.act, lambda: nc.scalar.activation(out=KSQ[:], in_=PQ[:], func=AF.Square),
          reads=[b_pq], writes=[b_ksq])
    kb.op(kb.pe, lambda: nc.tensor.matmul(PSS[:], lhsT=cx.onesb[:], rhs=KSQ[:], start=True, stop=True),
          reads=[b_ksq], writes=[b_pss])
    kb.op(kb.act, lambda: nc.scalar.activation(out=SD[:], in_=PSS[:], func=AF.Sqrt,
                                               bias=EPS_AP[0], scale=1.0 / HD),
          reads=[b_pss], writes=[b_sd])
    kb.op(kb.dve, lambda: nc.vector.reciprocal(out=RS[:], in_=SD[:]), reads=[b_sd], writes=[b_rs])
    kb.op(kb.dve, lambda: nc.vector.scalar_tensor_tensor(
        out=out_ap, in0=KF[:], scalar=gain_col, in1=RS[:], op0=ALU.mult, op1=ALU.mult),
        reads=[b_kf, b_rs], writes=[b_out])


def headnorm_tmp(kb, es, name):
    nc = kb.nc
    KF = es.enter_context(nc.sbuf_tensor(name + "KF", [128, 512], F32))
    KSQ = es.enter_context(nc.sbuf_tensor(name + "KSQ", [128, 512], BF16))
    SD = es.enter_context(nc.sbuf_tensor(name + "SD", [128, 512], F32))
    RS = es.enter_context(nc.sbuf_tensor(name + "RS", [128, 512], F32))
    PSS = es.enter_context(nc.psum_tensor(name + "PSS", [128, 512], F32))
    return (KF, KSQ, SD, RS, PSS, kb.buf("kf"), kb.buf("ksq"), kb.buf("sd"), kb.buf("rs"), kb.buf("pss"))


def std_tiles(h, NTT, tok0=0):
    return [(h[tok0 + i * 128: tok0 + (i + 1) * 128, :], i * 128, 128, 0) for i in range(NTT)]


def kv_phase(kb, cx, NT, tok0=0):
    nc = kb.nc
    NTT = NT // 128
    NQ = NT // 512
    h = cx.h
    with ExitStack() as es:
        XNT = es.enter_context(nc.sbuf_tensor("kXNT", [128, 16, NT], BF16))
        xb = {i * 128: kb.buf("kxnt%d" % i) for i in range(NTT)}
        norm_phase(kb, cx, std_tiles(h, NTT, tok0), cx.kv_norm, XNT, xb, "kn")
        xall = list(xb.values())
        WB = [es.enter_context(nc.sbuf_tensor("kWB%d" % i, [128, 16, 512], BF16)) for i in range(2)]
        WF = es.enter_context(nc.sbuf_tensor("kWF", [128, 16, 16], BF16))
        BFB = es.enter_context(nc.sbuf_tensor("kBFB", [128, 16], F32))
        KG = es.enter_context(nc.sbuf_tensor("kKG", [128, 1], F32))
        KTt = [es.enter_context(nc.sbuf_tensor("kKT%d" % i, [128, 512], BF16)) for i in range(2)]
        VTt = [es.enter_context(nc.sbuf_tensor("kVT%d" % i, [128, 512], BF16)) for i in range(2)]
        LF = es.enter_context(nc.sbuf_tensor("kLF", [128, NTT, 16], F32))
        F1 = es.enter_context(nc.sbuf_tensor("kF1", [128, 16], F32))
        E1 = es.enter_context(nc.sbuf_tensor("kE1", [128, 16], F32))
        CN = [es.enter_context(nc.sbuf_tensor("kCN%d" % i, [128, 16], F32)) for i in range(2)]
        PK = [es.enter_context(nc.psum_tensor("kPK%d" % i, [128, 512], F32)) for i in range(2)]
        PF = es.enter_context(nc.psum_tensor("kPF", [128, 512], F32))
        tmp = headnorm_tmp(kb, es, "k")
        b_wb = [kb.buf("kwb%d" % i, dma=True) for i in range(2)]
        b_misc = kb.buf("kmisc", dma=True)
        b_kt = [kb.buf("kkt%d" % i, dma=True) for i in range(2)]
        b_vt = [kb.buf("kvt%d" % i, dma=True) for i in range(2)]
        b_pk = [kb.buf("kpk%d" % i) for i in range(2)]
        b_pf = kb.buf("kpf")
        b_lf = kb.buf("klf")
        b_f1 = kb.buf("kf1")
        b_e1 = kb.buf("ke1")
        b_cn = [kb.buf("kcn%d" % i, dma=True) for i in range(2)]
        wv = cx.w_kvf.rearrange("(k p) n -> p k n", p=128)
        kb.dma(kb.pool, WF[:], wv[:, :, 2 * D:2 * D + 16], [], [b_misc], b_misc)
        kb.dma(kb.sp, BFB[:], cx.b_f.partition_broadcast(128), [], [b_misc], b_misc)
        kb.dma(kb.sp, KG[:], cx.k_norm.rearrange("(p o) -> p o", o=1), [], [b_misc], b_misc)
        blk = 0
        st = 0
        for q in range(4):
            ws = blk % 2
            blk += 1
            load_w(kb, WB[ws], wv[:, :, q * 512:(q + 1) * 512], b_wb[ws])
            for hh in range(4):
                head = 4 * q + hh
                for T in range(NQ):
                    s = st % 2
                    st += 1
                    kb.op(kb.pe, [(lambda k=k: nc.tensor.matmul(
                        PK[s][:], lhsT=WB[ws][:, k, hh * 128:(hh + 1) * 128],
                        rhs=XNT[:, k, T * 512:(T + 1) * 512], start=(k == 0), stop=(k == 15))) for k in range(16)],
                        reads=[b_wb[ws]] + xall, writes=[b_pk[s]])
                    headnorm(kb, cx, PK[s], b_pk[s], KG[:, 0:1], KTt[s][:], b_kt[s], tmp, "k")
                    kb.dma(kb.sp, cx.kt_out[head, :, tok0 + T * 512: tok0 + (T + 1) * 512], KTt[s][:], [b_kt[s], b_misc], [], b_kt[s])
        for j in range(4):
            ws = blk % 2
            blk += 1
            load_w(kb, WB[ws], wv[:, :, D + j * 512:D + (j + 1) * 512], b_wb[ws])
            for i in range(NTT):
                s = st % 2
                st += 1
                kb.op(kb.pe, [(lambda k=k: nc.tensor.matmul(
                    PK[s][:], lhsT=XNT[:, k, i * 128:(i + 1) * 128], rhs=WB[ws][:, k, :],
                    start=(k == 0), stop=(k == 15))) for k in range(16)],
                    reads=[b_wb[ws]] + xall, writes=[b_pk[s]])
                kb.op(kb.act, lambda: nc.scalar.copy(out=VTt[s][:], in_=PK[s][:]), reads=[b_pk[s]], writes=[b_vt[s]])
                kb.dma(kb.sp, cx.v_out[tok0 + i * 128: tok0 + (i + 1) * 128, j * 512:(j + 1) * 512], VTt[s][:], [b_vt[s]], [], b_vt[s])
        for i in range(NTT):
            kb.op(kb.pe, [(lambda k=k: nc.tensor.matmul(
                PF[:, 0:16], lhsT=XNT[:, k, i * 128:(i + 1) * 128], rhs=WF[:, k, :],
                start=(k == 0), stop=(k == 15))) for k in range(16)],
                reads=[b_misc] + xall, writes=[b_pf])
            kb.op(kb.dve, lambda: nc.vector.tensor_tensor(out=F1[:], in0=PF[:, 0:16], in1=BFB[:], op=ALU.add),
                  reads=[b_pf, b_misc], writes=[b_f1])
            kb.op(kb.act, lambda: nc.scalar.activation(out=E1[:], in_=F1[:], func=AF.Exp, scale=-1.0),
                  reads=[b_f1], writes=[b_e1])
            kb.op(kb.act, lambda: nc.scalar.activation(out=LF[:, i, :], in_=E1[:], func=AF.Ln, bias=cx.one_ap),
                  reads=[b_e1], writes=[b_lf])
        for i in range(NTT):
            s = i % 2
            fns = [(lambda jj=jj: nc.tensor.matmul(PF[:, 32:48], lhsT=cx.onesf[:], rhs=LF[:, jj, :],
                                                   start=(jj == 0), stop=False)) for jj in range(i)]
            fns.append(lambda: nc.tensor.matmul(PF[:, 32:48], lhsT=cx.trif[:], rhs=LF[:, i, :],
                                                start=(i == 0), stop=True))
            kb.op(kb.pe, fns, reads=[b_lf], writes=[b_pf])
            kb.op(kb.dve, lambda: nc.vector.tensor_scalar(out=CN[s][:], in0=PF[:, 32:48], scalar1=-1.0, scalar2=None,
                                                          op0=ALU.mult),
                  reads=[b_pf], writes=[b_cn[s]])
            kb.dma(kb.sp, cx.c_out[tok0 + i * 128: tok0 + (i + 1) * 128, :], CN[s][:], [b_cn[s]], [], b_cn[s])
        kb.barrier()


def attn_phase(kb, cx, j, NT, qbase=None):
    nc = kb.nc
    NTT = NT // 128
    NQ = NT // 512
    legacy = qbase is None
    if legacy:
        qbase = NT
    NK = qbase + NT
    NKB = NK // 128
    h = cx.h
    scale = float(HD) ** -0.5
    with ExitStack() as es0:
        OG = es0.enter_context(nc.sbuf_tensor("bOG", [128, 16, NT], BF16))
        b_og = [kb.buf("bog%d" % i) for i in range(NQ)]
        with ExitStack() as es:
            XNT = es.enter_context(nc.sbuf_tensor("bXNT", [128, 16, NT], BF16))
            xb = {i * 128: kb.buf("bxnt%d" % i) for i in range(NTT)}
            norm_phase(kb, cx, std_tiles(h, NTT, 0 if legacy else qbase), cx.b_norm[j], XNT, xb, "bn")
            xall = list(xb.values())
            CG = es.enter_context(nc.sbuf_tensor("bCG", [128, NKB, 16], F32))
            TOT = es.enter_context(nc.sbuf_tensor("bTOT", [128, 16], F32))
            CQ = es.enter_context(nc.sbuf_tensor("bCQ", [128, NQ, 16], F32))
            BIAS = es.enter_context(nc.sbuf_tensor("bBIAS", [128, NKB, NQ, 16], F32))
            QG = es.enter_context(nc.sbuf_tensor("bQG", [128, 1], F32))
            MF = es.enter_context(nc.sbuf_tensor("bMF", [128, 512], F32))
            MASK = es.enter_context(nc.sbuf_tensor("bMASK", [128, 4, 512], BF16))
            KTs = [es.enter_context(nc.sbuf_tensor("bKTs%d" % i, [128, NK], BF16)) for i in range(1)]
            Vs = [es.enter_context(nc.sbuf_tensor("bVs%d" % i, [128, NKB, 128], BF16)) for i in range(1)]
            WB = es.enter_context(nc.sbuf_tensor("bWB", [128, 16, 256], BF16))
            WG = es.enter_context(nc.sbuf_tensor("bWG", [128, 16, 256], BF16))
            QT = es.enter_context(nc.sbuf_tensor("bQT", [128, NT], BF16))
            GS = es.enter_context(nc.sbuf_tensor("bGS", [128, NT], BF16))
            PTt = [es.enter_context(nc.sbuf_tensor("bPT%d" % i, [128, 512], BF16)) for i in range(2)]
            RI = es.enter_context(nc.sbuf_tensor("bRI", [128, 512], F32))
            O1 = es.enter_context(nc.sbuf_tensor("bO1", [128, 512], F32))
            PQ = [es.enter_context(nc.psum_tensor("bPQ%d" % i, [128, 512], F32)) for i in range(2)]
            PS = [es.enter_context(nc.psum_tensor("bPS%d" % i, [128, 512], F32)) for i in range(2)]
            PO = es.enter_context(nc.psum_tensor("bPO", [128, 512], F32))
            PR = es.enter_context(nc.psum_tensor("bPR", [128, 512], F32))
            tmp = headnorm_tmp(kb, es, "b")
            b_c = kb.buf("bconst", dma=True)
            b_bias = kb.buf("bbias")
            b_mask = kb.buf("bmask")
            b_kv = [kb.buf("bkv%d" % i, dma=True) for i in range(1)]
            b_wb = kb.buf("bwb", dma=True)
            b_wg = kb.buf("bwg", dma=True)
            b_qt = kb.buf("bqt")
            b_gs = kb.buf("bgs")
            b_pt = [kb.buf("bpt%d" % i) for i in range(2)]
            b_pq = [kb.buf("bpq%d" % i) for i in range(2)]
            b_ps = [kb.buf("bps%d" % i) for i in range(2)]
            b_po = kb.buf("bpo")
            b_pr = kb.buf("bpr")
            b_ri = kb.buf("bri")
            b_o1 = kb.buf("bo1")
            kb.dma(kb.sp, CG[:], cx.c_all[0:NK, :].rearrange("(kb p) h -> p kb h", p=128), [], [b_c], b_c)
            if legacy:
                kb.dma(kb.sp, TOT[:], cx.ctot.partition_broadcast(128), [], [b_c], b_c)
            elif qbase > 0:
                kb.dma(kb.sp, TOT[:], cx.c_all[qbase - 1, :].partition_broadcast(128), [], [b_c], b_c)
            kb.dma(kb.sp, QG[:], cx.q_norm[j].rearrange("(p o) -> p o", o=1), [], [b_c], b_c)
            for T in range(NQ):
                kb.dma(kb.sp, CQ[:, T, :], cx.c_all[qbase + T * 512, :].partition_broadcast(128), [], [b_c], b_c)
            for kbi in range(qbase // 128 if qbase > 0 else NKB, NKB):
                kb.op(kb.dve, lambda: nc.vector.tensor_tensor(out=CG[:, kbi, :], in0=CG[:, kbi, :], in1=TOT[:], op=ALU.add),
                      reads=[b_c], writes=[b_c])
            for T in range(NQ):
                if qbase > 0:
                    kb.op(kb.dve, lambda: nc.vector.tensor_tensor(out=CQ[:, T, :], in0=CQ[:, T, :], in1=TOT[:], op=ALU.add),
                          reads=[b_c], writes=[b_c])
                kb.op(kb.dve, lambda: nc.vector.tensor_scalar(out=CQ[:, T, :], in0=CQ[:, T, :], scalar1=-SHIFT,
                                                              scalar2=None, op0=ALU.add),
                      reads=[b_c], writes=[b_c])
                for kbi in range(NKB):
                    kb.op(kb.dve, lambda: nc.vector.tensor_tensor(
                        out=BIAS[:, kbi, T, :], in0=CQ[:, T, :], in1=CG[:, kbi, :], op=ALU.subtract),
                        reads=[b_c], writes=[b_bias])
            for jd in range(4):
                kb.op(kb.pool, lambda: nc.gpsimd.memset(MF[:], 1.0), writes=[b_mask])
                kb.op(kb.pool, lambda: nc.gpsimd.affine_select(
                    out=MF[:], in_=MF[:], pattern=[[1, 512]], compare_op=ALU.is_ge, fill=0.0,
                    base=-128 * jd, channel_multiplier=-1), writes=[b_mask])
                kb.op(kb.dve, lambda: nc.vector.tensor_copy(out=MASK[:, jd, :], in_=MF[:]),
                      reads=[b_mask], writes=[b_mask])
            wv = cx.b_w_qg[j].rearrange("(k p) n -> p k n", p=128)
            vv = cx.v_all[0:NK, :].rearrange("(kb p) d -> p kb d", p=128)
            sq = 0
            s2 = 0
            for hp in range(8):
                load_w(kb, WB, wv[:, :, hp * 256:(hp + 1) * 256], b_wb)
                load_w(kb, WG, wv[:, :, D + hp * 256:D + (hp + 1) * 256], b_wg)
                for hh in range(2):
                    head = 2 * hp + hh
                    ks = 0
                    kb.dma(kb.sp, [KTs[ks][:], Vs[ks][:]], [cx.kt_all[head, :, 0:NK], vv[:, :, head * 128:(head + 1) * 128]],
                           [], [b_kv[ks]], b_kv[ks])
                    for T in range(NQ):
                        s = sq % 2
                        sq += 1
                        kb.op(kb.pe, [(lambda k=k: nc.tensor.matmul(
                            PQ[s][:], lhsT=WB[:, k, hh * 128:(hh + 1) * 128],
                            rhs=XNT[:, k, T * 512:(T + 1) * 512], start=(k == 0), stop=(k == 15))) for k in range(16)],
                            reads=[b_wb] + xall, writes=[b_pq[s]])
                        headnorm(kb, cx, PQ[s], b_pq[s], QG[:, 0:1], QT[:, T * 512:(T + 1) * 512], b_qt, tmp, "b")
                        s = sq % 2
                        sq += 1
                        kb.op(kb.pe, [(lambda k=k: nc.tensor.matmul(
                            PQ[s][:], lhsT=WG[:, k, hh * 128:(hh + 1) * 128],
                            rhs=XNT[:, k, T * 512:(T + 1) * 512], start=(k == 0), stop=(k == 15))) for k in range(16)],
                            reads=[b_wg] + xall, writes=[b_pq[s]])
                        kb.op(kb.act, lambda: nc.scalar.activation(out=GS[:, T * 512:(T + 1) * 512], in_=PQ[s][:],
                                                                   func=AF.Sigmoid),
                              reads=[b_pq[s]], writes=[b_gs])
                    for T in range(NQ):
                        nkb = (qbase + (T + 1) * 512) // 128
                        for kbi in range(nkb):
                            p = s2 % 2
                            s2 += 1
                            kb.op(kb.pe, lambda: nc.tensor.matmul(
                                PS[p][:], lhsT=KTs[ks][:, kbi * 128:(kbi + 1) * 128],
                                rhs=QT[:, T * 512:(T + 1) * 512], start=True, stop=True),
                                reads=[b_kv[ks], b_qt], writes=[b_ps[p]])
                            kb.op(kb.act, lambda: nc.scalar.activation(
                                out=PTt[p][:], in_=PS[p][:], func=AF.Exp,
                                bias=BIAS[:, kbi, T, head:head + 1], scale=scale),
                                reads=[b_ps[p], b_bias], writes=[b_pt[p]])
                            if kbi >= nkb - 4:
                                jd = kbi - (nkb - 4)
                                kb.op(kb.dve, lambda: nc.vector.tensor_tensor(
                                    out=PTt[p][:], in0=PTt[p][:], in1=MASK[:, jd, :], op=ALU.mult),
                                    reads=[b_pt[p], b_mask], writes=[b_pt[p]])
                            kb.op(kb.pe, [
                                lambda: nc.tensor.matmul(PO[:], lhsT=Vs[ks][:, kbi, :], rhs=PTt[p][:],
                                                         start=(kbi == 0), stop=(kbi == nkb - 1)),
                                lambda: nc.tensor.matmul(PR[:], lhsT=cx.onesb[:], rhs=PTt[p][:],
                                                         start=(kbi == 0), stop=(kbi == nkb - 1))],
                                reads=[b_kv[ks], b_pt[p]], writes=[b_po, b_pr])
                        kb.op(kb.dve, lambda: nc.vector.reciprocal(out=RI[:], in_=PR[:]), reads=[b_pr], writes=[b_ri])
                        kb.op(kb.dve, lambda: nc.vector.tensor_tensor(out=O1[:], in0=PO[:], in1=RI[:], op=ALU.mult),
                              reads=[b_po, b_ri], writes=[b_o1])
                        kb.op(kb.dve, lambda: nc.vector.tensor_tensor(
                            out=OG[:, head, T * 512:(T + 1) * 512], in0=O1[:], in1=GS[:, T * 512:(T + 1) * 512],
                            op=ALU.mult),
                            reads=[b_o1, b_gs], writes=[b_og[T]])
            kb.barrier()
        outproj_phase(kb, cx, OG, 16, NTT, cx.b_w_out[j], h, "bo", [], tok0=(0 if legacy else qbase))
    kb.barrier()


def gmlp_phase(kb, cx, l, NT, base=0):
    nc = kb.nc
    h = cx.h
    HT_ = NT // 2 if NT >= 1024 else NT
    for hf in range(NT // HT_):
        tok0 = base + hf * HT_
        NTT = HT_ // 128
        NQ = HT_ // 512
        with ExitStack() as es0:
            GATED = es0.enter_context(nc.sbuf_tensor("aGATED", [128, 16, HT_], BF16))
            b_gated = [kb.buf("agated%d" % i) for i in range(NQ)]
            with ExitStack() as es:
                XNT = es.enter_context(nc.sbuf_tensor("aXNT", [128, 16, HT_], BF16))
                xb = {i * 128: kb.buf("axnt%d" % i) for i in range(NTT)}
                tiles = [(h[tok0 + i * 128: tok0 + (i + 1) * 128, :], i * 128, 128, 0) for i in range(NTT)]
                norm_phase(kb, cx, tiles, cx.a_norm[l], XNT, xb, "an")
                xall = list(xb.values())
                V = es.enter_context(nc.sbuf_tensor("aV", [128, NTT, D], BF16))
                WB = [es.enter_context(nc.sbuf_tensor("aWB%d" % i, [128, 16, 512], BF16)) for i in range(2)]
                GV = es.enter_context(nc.sbuf_tensor("aGV", [128, D], F32))
                WSF = es.enter_context(nc.sbuf_tensor("aWSF", [128, 16, 128], F32))
                WST = es.enter_context(nc.sbuf_tensor("aWST", [128, 16, 128], BF16))
                BSR = es.enter_context(nc.sbuf_tensor("aBSR", [1, D], F32))
                JK = es.enter_context(nc.sbuf_tensor("aJK", [128, D], BF16))
                ST = es.enter_context(nc.sbuf_tensor("aST", [128, 4], F32))
                U = [es.enter_context(nc.sbuf_tensor("aU%d" % i, [128, 512], F32)) for i in range(2)]
                PA = [es.enter_context(nc.psum_tensor("aPA%d" % i, [128, 512], F32)) for i in range(2)]
                PM = [es.enter_context(nc.psum_tensor("aPM%d" % i, [128, 512], F32)) for i in range(2)]
                PW = es.enter_context(nc.psum_tensor("aPW", [128, 4, 128], F32))
                b_wb = [kb.buf("awb%d" % i, dma=True) for i in range(2)]
                b_c = kb.buf("aconst", dma=True)
                b_v = [kb.buf("av%d" % i) for i in range(NTT)]
                b_wst = kb.buf("awst")
                b_jk = kb.buf("ajk")
                b_st = [kb.buf("ast%d" % i) for i in range(3)]
                b_u = [kb.buf("au%d" % i) for i in range(2)]
                b_pa = [kb.buf("apa%d" % i) for i in range(2)]
                b_pm = [kb.buf("apm%d" % i) for i in range(2)]
                b_pw = kb.buf("apw")
                kb.dma(kb.sp, GV[:], cx.a_v_norm[l].partition_broadcast(128), [], [b_c], b_c)
                kb.dma(kb.sp, WSF[:], cx.a_w_s[l].rearrange("g t s -> t g s"), [], [b_c], b_c)
                kb.dma(kb.sp, BSR[:], cx.a_b_s[l].rearrange("(o g) t -> o (g t)", o=1), [], [b_c], b_c)
                for g4 in range(4):
                    kb.op(kb.pe, [(lambda gi=gi: nc.tensor.transpose(
                        out=PW[:, gi, :], in_=WSF[:, g4 * 4 + gi, :], identity=cx.identf[:])) for gi in range(4)],
                        reads=[b_c], writes=[b_pw])
                    for gi in range(4):
                        kb.op(kb.dve, lambda: nc.vector.tensor_tensor(
                            out=WST[:, g4 * 4 + gi, :], in0=PW[:, gi, :], in1=cx.trif[:], op=ALU.mult),
                            reads=[b_pw], writes=[b_wst])
                wv = cx.a_w_in[l].rearrange("(k p) n -> p k n", p=128)
                blk = 0
                st = 0
                for jb in range(4):
                    ws = blk % 2
                    blk += 1
                    load_w(kb, WB[ws], wv[:, :, D + jb * 512:D + (jb + 1) * 512], b_wb[ws])
                    for i in range(NTT):
                        s = st % 2
                        st += 1
                        kb.op(kb.pe, [(lambda k=k: nc.tensor.matmul(
                            PA[s][:], lhsT=XNT[:, k, i * 128:(i + 1) * 128], rhs=WB[ws][:, k, :],
                            start=(k == 0), stop=(k == 15))) for k in range(16)],
                            reads=[b_wb[ws]] + xall, writes=[b_pa[s]])
                        kb.op(kb.act, lambda: nc.scalar.activation(out=V[:, i, jb * 512:(jb + 1) * 512], in_=PA[s][:],
                                                                   func=AF.Gelu),
                              reads=[b_pa[s]], writes=[b_v[i]])
                for i in range(NTT):
                    kb.op(kb.act, lambda: nc.scalar.activation(out=JK[:], in_=V[:, i, :], func=AF.Square,
                                                               accum_out=ST[:, 0:1]),
                          reads=[b_v[i]], writes=[b_jk, b_st[0]])
                    rstd_from_ss(kb, ST[:, 0:1], ST[:, 1:2], ST[:, 2:3], D, b_st[0], b_st[1], b_st[2])
                    kb.op(kb.dve, lambda: nc.vector.scalar_tensor_tensor(
                        out=V[:, i, :], in0=V[:, i, :], scalar=ST[:, 2:3], in1=GV[:], op0=ALU.mult, op1=ALU.mult),
                        reads=[b_v[i], b_st[2], b_c], writes=[b_v[i]])
                for q in range(4):
                    ws = blk % 2
                    blk += 1
                    load_w(kb, WB[ws], wv[:, :, q * 512:(q + 1) * 512], b_wb[ws])
                    for gi in range(4):
                        g = 4 * q + gi
                        for T in range(NQ):
                            s = st % 2
                            st += 1
                            kb.op(kb.pe, [(lambda k=k: nc.tensor.matmul(
                                PA[s][:], lhsT=WB[ws][:, k, gi * 128:(gi + 1) * 128],
                                rhs=XNT[:, k, T * 512:(T + 1) * 512], start=(k == 0), stop=(k == 15))) for k in range(16)],
                                reads=[b_wb[ws]] + xall, writes=[b_pa[s]])
                            kb.op(kb.act, lambda: nc.scalar.activation(out=U[s][:], in_=PA[s][:], func=AF.Gelu),
                                  reads=[b_pa[s]], writes=[b_u[s]])
                            fns = []
                            for n in range(4):
                                fns.append(lambda n=n: nc.tensor.matmul(
                                    PM[s][:, n * 128:(n + 1) * 128], lhsT=cx.onesf[0:1, :],
                                    rhs=BSR[0:1, g * 128:(g + 1) * 128], start=True, stop=False))
                                fns.append(lambda n=n: nc.tensor.matmul(
                                    PM[s][:, n * 128:(n + 1) * 128],
                                    lhsT=V[:, 4 * T + n, g * 128:(g + 1) * 128], rhs=WST[:, g, :],
                                    start=False, stop=True))
                            kb.op(kb.pe, fns, reads=[b_c, b_wst] + [b_v[4 * T + n] for n in range(4)],
                                  writes=[b_pm[s]])
                            kb.op(kb.dve, lambda: nc.vector.tensor_tensor(
                                out=GATED[:, g, T * 512:(T + 1) * 512], in0=PM[s][:], in1=U[s][:], op=ALU.mult),
                                reads=[b_pm[s], b_u[s]], writes=[b_gated[T]])
                kb.barrier()
            outproj_phase(kb, cx, GATED, 16, NTT, cx.a_w_out[l], h, "ao", [], tok0=tok0)
        kb.barrier()


def final_phase(kb, cx, NT, tok0=0):
    nc = kb.nc
    NTT = NT // 128
    h = cx.h
    with ExitStack() as es:
        GB = es.enter_context(nc.sbuf_tensor("zGB", [128, D], F32))
        HT = [es.enter_context(nc.sbuf_tensor("zHT%d" % i, [128, D], F32)) for i in range(2)]
        OT = [es.enter_context(nc.sbuf_tensor("zOT%d" % i, [128, D], F32)) for i in range(2)]
        JK = es.enter_context(nc.sbuf_tensor("zJK", [128, D], BF16))
        ST = [es.enter_context(nc.sbuf_tensor("zST%d" % i, [128, 4], F32)) for i in range(2)]
        b_gb = kb.buf("zgb", dma=True)
        b_ht = [kb.buf("zht%d" % i, dma=True) for i in range(2)]
        b_ot = [kb.buf("zot%d" % i, dma=True) for i in range(2)]
        b_jk = kb.buf("zjk")
        b_st = [[kb.buf("zst%d_%d" % (i, j)) for j in range(3)] for i in range(2)]
        kb.dma(kb.sp, GB[:], cx.final_norm.partition_broadcast(128), [], [b_gb], b_gb)
        for i in range(NTT):
            s = i % 2
            kb.dma(kb.sp, HT[s][:], h[tok0 + i * 128: tok0 + (i + 1) * 128, :], [], [b_ht[s]], b_ht[s])
            kb.op(kb.act, lambda: nc.scalar.activation(out=JK[:], in_=HT[s][:], func=AF.Square,
                                                       accum_out=ST[s][:, 0:1]),
                  reads=[b_ht[s]], writes=[b_jk, b_st[s][0]])
            rstd_from_ss(kb, ST[s][:, 0:1], ST[s][:, 1:2], ST[s][:, 2:3], D, b_st[s][0], b_st[s][1], b_st[s][2])
            kb.op(kb.dve, lambda: nc.vector.scalar_tensor_tensor(
                out=OT[s][:], in0=HT[s][:], scalar=ST[s][:, 2:3], in1=GB[:], op0=ALU.mult, op1=ALU.mult),
                reads=[b_ht[s], b_st[s][2], b_gb], writes=[b_ot[s]])
            kb.dma(kb.sp, cx.out[tok0 + i * 128: tok0 + (i + 1) * 128, :], OT[s][:], [b_ot[s]], [], b_ot[s])
        kb.barrier()


class NCProxy:
    def __init__(self, nc):
        self._nc = nc
        self._cnt = 0

    def __getattr__(self, k):
        return getattr(self._nc, k)

    def sbuf_tensor(self, name, shape, dt):
        self._cnt += 1
        return self._nc.sbuf_tensor("%s_%d" % (name, self._cnt), shape, dt)

    def psum_tensor(self, name, shape, dt):
        self._cnt += 1
        return self._nc.psum_tensor("%s_%d" % (name, self._cnt), shape, dt)


def build_program(phases, NT, NLF=4, NLA=2, NLB=2):
    nc_real = bass.Bass("TRN2", target_bir_lowering=False)
    nc = NCProxy(nc_real)
    cx = Ctx()

    def din(name, shape, dt=F32):
        return nc.dram_tensor(name, list(shape), dt, kind="ExternalInput").ap()

    need = set(p[0] for p in phases)
    cx.h_in = din("h_in", [NT, D])
    cx.h = nc.dram_tensor("h", [NT, D], F32, kind="ExternalOutput").ap()
    if "A" in need:
        cx.a_norm = din("a_norm", [NLA, D])
        cx.a_w_in = din("a_w_in", [NLA, D, 2 * D])
        cx.a_v_norm = din("a_v_norm", [NLA, D])
        cx.a_w_s = din("a_w_s", [NLA, 16, 128, 128])
        cx.a_b_s = din("a_b_s", [NLA, 16, 128])
        cx.a_w_out = din("a_w_out", [NLA, D, D])
    if "KV" in need:
        cx.kv_norm = din("kv_norm", [D])
        cx.w_kvf = din("w_kvf", [D, 2 * D + 16])
        cx.b_f = din("b_f", [16])
        cx.k_norm = din("k_norm", [128])
        cx.kt_out = nc.dram_tensor("kt_out", [16, 128, NT], BF16, kind="ExternalOutput").ap()
        cx.v_out = nc.dram_tensor("v_out", [NT, D], BF16, kind="ExternalOutput").ap()
        cx.c_out = nc.dram_tensor("c_out", [NT, 16], F32, kind="ExternalOutput").ap()
    if "B" in need:
        cx.b_norm = din("b_norm", [NLB, D])
        cx.b_w_qg = din("b_w_qg", [NLB, D, 2 * D])
        cx.q_norm = din("q_norm", [NLB, 128])
        cx.b_w_out = din("b_w_out", [NLB, D, D])
        cx.kt_all = din("kt_all", [16, 128, 2 * NT], BF16)
        cx.v_all = din("v_all", [2 * NT, D], BF16)
        cx.c_all = din("c_all", [2 * NT, 16])
        cx.ctot = din("ctot", [16])
    if "Z" in need:
        cx.final_norm = din("final_norm", [D])
        cx.out = nc.dram_tensor("out", [NT, D], F32, kind="ExternalOutput").ap()
    if "F" in need:
        cx.f_norm = din("f_norm", [NLF, D])
        cx.f_w_up = din("f_w_up", [NLF, D, 2 * FF])
        cx.f_w_down = din("f_w_down", [NLF, FF, D])
        cx.f_cw = din("f_cw", [NLF, 128, 2 * NFC, 3])
        cx.f_cb = din("f_cb", [NLF, 128, 2 * NFC])
        cx.halo = din("halo", [NLF, 128, D])
    with ExitStack() as es:
        kb = KB(nc, es)
        epsT = es.enter_context(nc.sbuf_tensor("epsT", [128, 2], F32))
        EPS_AP[0] = epsT[:, 0:1]
        cx.one_ap = epsT[:, 1:2]
        b_e = kb.buf("eps")
        kb.op(kb.pool, lambda: nc.gpsimd.memset(epsT[:, 0:1], EPS), writes=[b_e])
        kb.op(kb.pool, lambda: nc.gpsimd.memset(epsT[:, 1:2], 1.0), writes=[b_e])
        setup_consts(kb, cx, es)
        b_h = kb.buf("hcopy", dma=True)
        nsp = max(1, NT // 512)
        for i in range(nsp):
            r = slice(i * NT // nsp, (i + 1) * NT // nsp)
            kb.dma(kb.sp, cx.h[r, :], cx.h_in[r, :], [], [b_h], b_h)
        kb.barrier()
        for p in phases:
            if p[0] == "F":
                ffn_phase(kb, cx, p[1], NT)
            elif p[0] == "A":
                gmlp_phase(kb, cx, p[1], NT)
            elif p[0] == "KV":
                kv_phase(kb, cx, NT)
            elif p[0] == "B":
                attn_phase(kb, cx, p[1], NT)
            elif p[0] == "Z":
                final_phase(kb, cx, NT)
        kb.barrier()
    return nc_real


def build_fused(S=4096, NT=2048):
    nc_real = bass.Bass("TRN2", target_bir_lowering=False)
    nc = NCProxy(nc_real)
    cx = Ctx()

    def din(name, shape, dt=F32):
        return nc.dram_tensor(name, list(shape), dt, kind="ExternalInput").ap()

    def scr(name, shape, dt=F32):
        return nc.dram_tensor(name, list(shape), dt, kind="Internal").ap()

    cx.x = din("x", [S, D])
    cx.zrows = din("zrows", [128, D])
    cx.a_norm = din("a_norm", [2, D])
    cx.a_w_in = din("a_w_in", [2, D, 2 * D])
    cx.a_v_norm = din("a_v_norm", [2, D])
    cx.a_w_s = din("a_w_s", [2, 16, 128, 128])
    cx.a_b_s = din("a_b_s", [2, 16, 128])
    cx.a_w_out = din("a_w_out", [2, D, D])
    cx.kv_norm = din("kv_norm", [D])
    cx.w_kvf = din("w_kvf", [D, 2 * D + 16])
    cx.b_f = din("b_f", [16])
    cx.k_norm = din("k_norm", [128])
    cx.b_norm = din("b_norm", [2, D])
    cx.b_w_qg = din("b_w_qg", [2, D, 2 * D])
    cx.q_norm = din("q_norm", [2, 128])
    cx.b_w_out = din("b_w_out", [2, D, D])
    cx.f_norm = din("f_norm", [4, D])
    cx.f_w_up = din("f_w_up", [4, D, 2 * FF])
    cx.f_w_down = din("f_w_down", [4, FF, D])
    cx.f_cw = din("f_cw", [4, 128, 2 * NFC, 3])
    cx.f_cb = din("f_cb", [4, 128, 2 * NFC])
    cx.final_norm = din("final_norm", [D])
    cx.out = nc.dram_tensor("out", [S, D], F32, kind="ExternalOutput").ap()
    cx.h = scr("h_scr", [S, D])
    cx.kt_out = cx.kt_all = scr("kt_scr", [16, 128, S], BF16)
    cx.v_out = cx.v_all = scr("v_scr", [S, D], BF16)
    cx.c_out = cx.c_all = scr("c_scr", [S, 16])
    bases = list(range(0, S, NT))
    with ExitStack() as es:
        kb = KB(nc, es)
        epsT = es.enter_context(nc.sbuf_tensor("epsT", [128, 2], F32))
        EPS_AP[0] = epsT[:, 0:1]
        cx.one_ap = epsT[:, 1:2]
        b_e = kb.buf("eps")
        kb.op(kb.pool, lambda: nc.gpsimd.memset(epsT[:, 0:1], EPS), writes=[b_e])
        kb.op(kb.pool, lambda: nc.gpsimd.memset(epsT[:, 1:2], 1.0), writes=[b_e])
        setup_consts(kb, cx, es)
        b_h = kb.buf("hcopy", dma=True)
        nsp = S // 512
        kb.dma(kb.sp, [cx.h[i * 512:(i + 1) * 512, :] for i in range(nsp)],
               [cx.x[i * 512:(i + 1) * 512, :] for i in range(nsp)], [], [b_h], b_h)
        kb.barrier()

        def ffn_all(l):
            for b in reversed(bases):
                halo = (cx.zrows, 0) if b == 0 else (cx.h[b - 128:b, :], 126)
                ffn_phase(kb, cx, l, NT, tok0=b, halo=halo)

        for l in range(2):
            for b in bases:
                gmlp_phase(kb, cx, l, NT, base=b)
            ffn_all(l)
        for b in bases:
            kv_phase(kb, cx, NT, tok0=b)
        for j in range(2):
            for b in bases:
                attn_phase(kb, cx, j, NT, qbase=b)
            ffn_all(2 + j)
        for b in bases:
            final_phase(kb, cx, NT, tok0=b)
        kb.barrier()
    return nc_real


def _layout_inputs(inp):
    cw = np.ascontiguousarray(np.asarray(inp["f_conv_w"]).reshape(4, 3, 2 * NFC, 128).transpose(0, 3, 2, 1))
    cb = np.ascontiguousarray(np.asarray(inp["f_conv_b"]).reshape(4, 2 * NFC, 128).transpose(0, 2, 1))
    com = {k: inp[k] for k in ["a_norm", "a_w_in", "a_v_norm", "a_w_s", "a_b_s", "a_w_out", "kv_norm", "w_kvf",
                               "b_f", "k_norm", "b_norm", "b_w_qg", "q_norm", "b_w_out", "f_norm", "f_w_up",
                               "f_w_down", "final_norm"]}
    com["f_cw"] = cw
    com["f_cb"] = cb
    com["zrows"] = np.zeros((128, D), np.float32)
    return com


def kernel(**inputs):
    inp = {k: np.ascontiguousarray(np.asarray(v)) for k, v in inputs.items()}
    x = inp["x"]
    B, S = x.shape[0], x.shape[1]
    com = _layout_inputs(inp)
    nc = build_fused(S=S, NT=min(2048, S))
    in_maps = [dict(x=np.ascontiguousarray(x[b]), **com) for b in range(B)]
    res = run_bass_kernel_spmd(nc, in_maps, core_ids=list(range(B)))
    return np.stack([np.asarray(res.results[b]["out"]) for b in range(B)], axis=0)
```
